# Optimizing a Trainium2 kernel written in Bass

```python
import jax, jax.numpy as jnp
from jax import lax
import numpy as np

D_MODEL = 2048
BATCH = 8
SEQ = 4096
DEPTH = 2

CHUNK = 64
N_PREV = 8
N_BAND = N_PREV + 1
MEM_LEN = 256
HEAD_DIM = 128
W_A = 512
CONV_WIDTH = 3
H_B = 8
W_B = H_B * HEAD_DIM
H_M = 4
W_M = H_M * HEAD_DIM
N_BRANCH = 3
MAX_REL = 256
EPS = 1e-6
SPLIT_SIZES = (W_A, W_A, W_A, W_A,
               W_B, W_B, W_B, W_B,
               W_M, W_M,
               D_MODEL, D_MODEL, D_MODEL)
N_IN = sum(SPLIT_SIZES)

kernel_name = "hybrid_gated_conv_chunkattn_memory"


def rms_norm(x, g):
    xf = x.astype(jnp.float32)
    y = xf * lax.rsqrt(jnp.mean(xf * xf, axis=-1, keepdims=True) + EPS)
    return (y * g.astype(jnp.float32)).astype(x.dtype)


def short_conv_mixer(b, c, xin, conv_w):
    u = c * xin
    s = u.shape[1]
    up = jnp.pad(u, ((0, 0), (CONV_WIDTH - 1, 0), (0, 0)))
    y = up[:, 0:s] * conv_w[0]
    for tap in range(1, CONV_WIDTH):
        y = y + up[:, tap:tap + s] * conv_w[tap]
    return b * y


def relative_bias(rel_table):
    j = jnp.arange(N_BAND, dtype=jnp.int32)[:, None, None]
    iq = jnp.arange(CHUNK, dtype=jnp.int32)[None, :, None]
    ik = jnp.arange(CHUNK, dtype=jnp.int32)[None, None, :]
    rel = (N_PREV - j) * CHUNK + iq - ik
    idx = jnp.clip(rel, -MAX_REL, MAX_REL) + MAX_REL
    bias = rel_table[idx]
    return jnp.transpose(bias, (3, 1, 0, 2))[:, None]


def chunk_band_attention(q, k, v, q_g, k_g, rel_table):
    bsz, s, h, dh = q.shape
    nc = s // CHUNK
    q = rms_norm(q, q_g) * (dh ** -0.5)
    k = rms_norm(k, k_g)
    qc = q.reshape(bsz, nc, CHUNK, h, dh)
    pad = ((0, 0), (N_PREV, 0), (0, 0), (0, 0), (0, 0))
    kp = jnp.pad(k.reshape(bsz, nc, CHUNK, h, dh), pad)
    vp = jnp.pad(v.reshape(bsz, nc, CHUNK, h, dh), pad)
    scores = jnp.stack([jnp.einsum('bnqhd,bnkhd->bhnqk', qc, kp[:, j:j + nc])
                        for j in range(N_BAND)], axis=4).astype(jnp.float32)
    scores = scores + relative_bias(rel_table).astype(jnp.float32)
    valid = (jnp.arange(nc)[:, None] - N_PREV + jnp.arange(N_BAND)[None, :]) >= 0
    scores = jnp.where(valid.reshape(1, 1, nc, 1, N_BAND, 1), scores, -1e30)
    p = jax.nn.softmax(scores.reshape(bsz, h, nc, CHUNK, N_BAND * CHUNK), axis=-1)
    p = p.reshape(bsz, h, nc, CHUNK, N_BAND, CHUNK).astype(v.dtype)
    out = jnp.einsum('bhnqk,bnkhd->bnqhd', p[:, :, :, :, 0], vp[:, 0:nc])
    for j in range(1, N_BAND):
        out = out + jnp.einsum('bhnqk,bnkhd->bnqhd', p[:, :, :, :, j], vp[:, j:j + nc])
    return out.reshape(bsz, s, h * dh)


def memory_attention(q, mk, mv, q_g, k_g):
    bsz, s, h, dh = q.shape
    q = rms_norm(q, q_g) * (dh ** -0.5)
    mk = rms_norm(mk, k_g)
    sc = jnp.einsum('bshd,bmhd->bhsm', q, mk).astype(jnp.float32)
    p = jax.nn.softmax(sc, axis=-1).astype(mv.dtype)
    return jnp.einsum('bhsm,bmhd->bshd', p, mv).reshape(bsz, s, h * dh)


def setup_inputs(seed: int = 0) -> dict:
    key = jax.random.key(seed)
    ks = jax.random.split(key, 20)
    f32 = jnp.float32
    nrm = lambda k, shape, scale: jax.random.normal(k, shape, f32) * scale
    return {
        "x": nrm(ks[0], (BATCH, SEQ, D_MODEL), 1.0),
        "mem": nrm(ks[1], (BATCH, MEM_LEN, D_MODEL), 1.0),
        "norm_g": 1.0 + nrm(ks[2], (DEPTH, D_MODEL), 0.02),
        "w_in": nrm(ks[3], (DEPTH, D_MODEL, N_IN), D_MODEL ** -0.5),
        "b_gate": nrm(ks[4], (DEPTH, N_BRANCH, D_MODEL), 0.02),
        "conv_w": nrm(ks[5], (DEPTH, CONV_WIDTH, W_A), CONV_WIDTH ** -0.5),
        "q_norm_g": 1.0 + nrm(ks[6], (DEPTH, HEAD_DIM), 0.02),
        "k_norm_g": 1.0 + nrm(ks[7], (DEPTH, HEAD_DIM), 0.02),
        "rel_table": nrm(ks[8], (DEPTH, 2 * MAX_REL + 1, H_B), 0.5),
        "mem_norm_g": 1.0 + nrm(ks[9], (DEPTH, D_MODEL), 0.02),
        "w_mem_kv": nrm(ks[10], (DEPTH, D_MODEL, 2 * W_M), D_MODEL ** -0.5),
        "mq_norm_g": 1.0 + nrm(ks[11], (DEPTH, HEAD_DIM), 0.02),
        "mk_norm_g": 1.0 + nrm(ks[12], (DEPTH, HEAD_DIM), 0.02),
        "w_branch_a": nrm(ks[13], (DEPTH, W_A, D_MODEL), W_A ** -0.5),
        "w_branch_b": nrm(ks[14], (DEPTH, W_B, D_MODEL), W_B ** -0.5),
        "w_branch_m": nrm(ks[15], (DEPTH, W_M, D_MODEL), W_M ** -0.5),
        "w_out": nrm(ks[16], (DEPTH, D_MODEL, D_MODEL), (2.0 * DEPTH * D_MODEL) ** -0.5),
    }


def reference(x, mem, norm_g, w_in, b_gate, conv_w, q_norm_g, k_norm_g, rel_table,
              mem_norm_g, w_mem_kv, mq_norm_g, mk_norm_g, w_branch_a, w_branch_b,
              w_branch_m, w_out):
    bsz, s, _ = x.shape
    offsets = np.cumsum(SPLIT_SIZES)[:-1].tolist()
    for l in range(DEPTH):
        h = rms_norm(x, norm_g[l])
        proj = h @ w_in[l]
        (a_b, a_c, a_x, a_z, q, k, v, b_z, mq, m_z,
         g_a, g_b, g_m) = jnp.split(proj, offsets, axis=-1)

        ya = short_conv_mixer(a_b, a_c, a_x, conv_w[l]) * jax.nn.silu(a_z)
        ya = ya @ w_branch_a[l]

        yb = chunk_band_attention(q.reshape(bsz, s, H_B, HEAD_DIM),
                                  k.reshape(bsz, s, H_B, HEAD_DIM),
                                  v.reshape(bsz, s, H_B, HEAD_DIM),
                                  q_norm_g[l], k_norm_g[l], rel_table[l])
        yb = (yb * jax.nn.silu(b_z)) @ w_branch_b[l]

        mkv = rms_norm(mem, mem_norm_g[l]) @ w_mem_kv[l]
        mk, mv = jnp.split(mkv, 2, axis=-1)
        ym = memory_attention(mq.reshape(bsz, s, H_M, HEAD_DIM),
                              mk.reshape(bsz, MEM_LEN, H_M, HEAD_DIM),
                              mv.reshape(bsz, MEM_LEN, H_M, HEAD_DIM),
                              mq_norm_g[l], mk_norm_g[l])
        ym = (ym * jax.nn.silu(m_z)) @ w_branch_m[l]

        merged = (jax.nn.sigmoid(g_a + b_gate[l, 0]) * ya
                  + jax.nn.sigmoid(g_b + b_gate[l, 1]) * yb
                  + jax.nn.sigmoid(g_m + b_gate[l, 2]) * ym)
        x = x + merged @ w_out[l]
    return x
```

```python
import numpy as np
import concourse.bass as bass
import concourse.mybir as mybir
from concourse.bass_utils import run_bass_kernel_spmd

F32 = mybir.dt.float32
BF16 = mybir.dt.bfloat16
AF = mybir.ActivationFunctionType
ALU = mybir.AluOpType

D = 2048
KC = 16
T = 512
NSUB = 4
DEPTH = 2
N_IN = 13312
EPS = 1e-6
OFF = dict(a_b=0, a_c=512, a_x=1024, a_z=1536, q=2048, k=3072, v=4096, b_z=5120,
           mq=6144, m_z=6656, g_a=7168, g_b=9216, g_m=11264)
NB = 4
GW = 256
NGT = 68
NGS = 4
NGL = NGT + NGS
MASKV = -100.0
V_BG, V_CW, V_QG, V_KG, V_MQG, V_MKG, NV = 0, 48, 60, 61, 62, 63, 64
DV_NBG, DV_QG, DV_MQG, NDV = 0, 48, 49, 50


class Sched:
    def __init__(self, nc):
        self.nc = nc
        self.ops = []
        self.last_w = {}
        self.readers = {}
        self.ch_count = {}
        self.ch_last = {}

    def _add(self, eng, fn, reads, writes, extra, is_dma=False, ch=None):
        deps = set(extra)
        for k in list(reads) + list(writes):
            w = self.last_w.get(k)
            if w is not None:
                deps.add(w)
        for k in writes:
            for r in self.readers.get(k, ()):
                deps.add(r)
        idx = len(self.ops)
        deps.discard(idx)
        op = dict(eng=eng, fn=fn, deps=deps, is_dma=is_dma, ch=ch, mark=False, val=None)
        if is_dma:
            self.ch_count[ch] = self.ch_count.get(ch, 0) + 1
            op["val"] = 16 * self.ch_count[ch]
        self.ops.append(op)
        for k in writes:
            self.last_w[k] = idx
            self.readers[k] = []
        for k in reads:
            if k not in writes:
                self.readers.setdefault(k, []).append(idx)
        return idx

    def op(self, eng, fn, reads=(), writes=(), extra=()):
        return self._add(eng, fn, reads, writes, extra)

    def dma(self, eng, out, in_, ch, reads=(), writes=(), extra=(), chain=True):
        q = {"sp": self.nc.sync, "act": self.nc.scalar, "pool": self.nc.gpsimd}[eng]
        extra = tuple(extra)
        if chain and ch in self.ch_last:
            extra = extra + (self.ch_last[ch],)
        idx = self._add(eng, lambda: q.dma_start(out=out, in_=in_), reads, writes, extra,
                        is_dma=True, ch=ch)
        self.ch_last[ch] = idx
        return idx

    def emit(self, sems, ch_sems):
        nc = self.nc
        engs = {"pe": nc.tensor, "act": nc.scalar, "dve": nc.vector, "pool": nc.gpsimd, "sp": nc.sync}
        ops = self.ops
        for op in ops:
            for d in op["deps"]:
                dop = ops[d]
                if dop["is_dma"]:
                    continue
                if dop["eng"] == "pe" and op["eng"] == "pe" and not op["is_dma"]:
                    continue
                dop["mark"] = True
        cnt = {e: 0 for e in engs}
        seen = {e: {} for e in engs}
        for op in ops:
            e = op["eng"]
            waits = {}
            for d in op["deps"]:
                dop = ops[d]
                if dop["is_dma"]:
                    key = ("ch", dop["ch"])
                    sem = ch_sems[dop["ch"]]
                    v = dop["val"]
                else:
                    if dop["eng"] == "pe" and e == "pe" and not op["is_dma"]:
                        continue
                    key = ("e", dop["eng"])
                    sem = sems[dop["eng"]]
                    v = dop["val"]
                    assert v is not None
                if v <= seen[e].get(key, 0):
                    continue
                if key not in waits or waits[key][1] < v:
                    waits[key] = (sem, v)
            for key, (sem, v) in waits.items():
                engs[e].wait_ge(sem, v)
                seen[e][key] = v
            ins = op["fn"]()
            if op["is_dma"]:
                ins.then_inc(ch_sems[op["ch"]], 16)
            elif op["mark"]:
                cnt[e] += 1
                op["val"] = cnt[e]
                ins.then_inc(sems[e], 1)
        return cnt


def _group_table(l):
    groups = []

    def win(c0, n, dcol=0):
        return ("w_in", 0, D, c0, n, 0, dcol)

    for j in range(4):
        groups.append([win(OFF["a_c"] + 128 * j, 128, 0), win(OFF["a_x"] + 128 * j, 128, 128)])
        groups.append([win(OFF["a_b"] + 128 * j, 128, 0), win(OFF["a_z"] + 128 * j, 128, 128)])
    for name in ("q", "k", "v", "b_z"):
        for i in range(4):
            groups.append([win(OFF[name] + 256 * i, 256)])
    for name in ("mq", "m_z"):
        for i in range(2):
            groups.append([win(OFF[name] + 256 * i, 256)])
    for f in range(16):
        groups.append([win(OFF["g_a"] + 128 * f, 128, 0), win(OFF["g_b"] + 128 * f, 128, 128)])
        groups.append([win(OFF["g_m"] + 128 * f, 128, 0),
                       ("w_branch_a", 0, 512, 128 * f, 128, 0, 128),
                       ("w_branch_b", 0, 1024, 128 * f, 128, 4, 128),
                       ("w_branch_m", 0, 512, 128 * f, 128, 12, 128)])
    for i in range(8):
        groups.append([("w_out", 0, D, 256 * i, 256, 0, 0)])
    assert len(groups) == NGT
    for i in range(4):
        groups.append([("w_mem_kv", 0, D, 256 * i, 256, 0, 0)])
    return groups


G_A = 0
G_Q, G_K, G_V, G_BZ = 8, 12, 16, 20
G_MQ, G_MZ = 24, 26
G_M = 28
G_O = 60
G_MKV = 68


def build_program(SEQ, NL=2):
    NT = SEQ // T
    layers = tuple(range(NL))
    DEPTH = NL
    nc = bass.Bass("TRN2", target_bir_lowering=False)
    dr = {}
    dr["x"] = nc.dram_tensor("x", [SEQ, D], F32, kind="ExternalInput").ap()
    dr["mem"] = nc.dram_tensor("mem", [256, D], F32, kind="ExternalInput").ap()
    dr["w_in"] = nc.dram_tensor("w_in", [DEPTH, D, N_IN], F32, kind="ExternalInput").ap()
    dr["w_mem_kv"] = nc.dram_tensor("w_mem_kv", [DEPTH, D, 1024], F32, kind="ExternalInput").ap()
    dr["w_branch_a"] = nc.dram_tensor("w_branch_a", [DEPTH, 512, D], F32, kind="ExternalInput").ap()
    dr["w_branch_b"] = nc.dram_tensor("w_branch_b", [DEPTH, 1024, D], F32, kind="ExternalInput").ap()
    dr["w_branch_m"] = nc.dram_tensor("w_branch_m", [DEPTH, 512, D], F32, kind="ExternalInput").ap()
    dr["w_out"] = nc.dram_tensor("w_out", [DEPTH, D, D], F32, kind="ExternalInput").ap()
    dr["vecs"] = nc.dram_tensor("vecs", [128, DEPTH * NV], F32, kind="ExternalInput").ap()
    dr["grows"] = nc.dram_tensor("grows", [DEPTH * 2, D], F32, kind="ExternalInput").ap()
    dr["ident"] = nc.dram_tensor("ident", [128, 128], F32, kind="ExternalInput").ap()
    dr["biasT"] = nc.dram_tensor("biasT", [DEPTH, 8, 128, 5 * 128], F32, kind="ExternalInput").ap()
    dr["out"] = nc.dram_tensor("out", [SEQ, D], F32, kind="ExternalOutput").ap()
    x1 = nc.dram_tensor("x1", [SEQ, D], F32, kind="Internal").ap()
    wsc = nc.dram_tensor("wsc", [DEPTH, NGL, 128, KC * GW], BF16, kind="Internal").ap()

    from contextlib import ExitStack
    with ExitStack() as es:
        def sb(name, shape, dt):
            return es.enter_context(nc.sbuf_tensor(name, shape, dt))

        vecs = sb("vecs_sb", [128, DEPTH * NV], F32)
        dv = sb("dv", [128, DEPTH * NDV], F32)
        grow = sb("grow_sb", [128, D], F32)
        ident = sb("ident_sb", [128, 128], BF16)
        identf = sb("identf", [128, 128], F32)
        onesm = sb("onesm", [128, 128], BF16)
        ones1 = sb("ones1", [128, 128], BF16)
        xs = sb("xs", [128, 2, D], F32)
        R = sb("R", [128, 16, 512], BF16)
        hT = sb("hT", [128, KC, T], BF16)
        XN = sb("XN", [128, 16, 512], BF16)
        kT = sb("kT", [128, 2, 8, T], BF16)
        vv = sb("vv", [128, 2, NSUB, 1024], BF16)
        qT = sb("qT", [128, 8, T], BF16)
        yA = sb("yA", [128, 4, T], BF16)
        BT = sb("BT", [128, 8, 5 * 128], BF16)
        mkT = sb("mkT", [128, 4, 256], BF16)
        mv = sb("mv", [128, 2, 512], BF16)
        wb = sb("wb", [128, NB, KC, GW], BF16)
        NTMP = 7
        tmp = sb("tmp", [128, NTMP, 520], F32)
        NSQ = 2
        sqb = sb("sqb", [128, NSQ, T], BF16)
        PT = sb("PT", [128, 2, 2560], BF16)
        osl = sb("osl", [128, 2, NSUB, GW], F32)
        ucar = sb("ucar", [128, 4, 2], F32)
        mgx = sb("mgx", [128, 4, T], BF16)
        stat = sb("stat", [128, 8], F32)
        ps = es.enter_context(nc.psum_tensor("ps", [128, 8 * 512], F32))

        eng_names = ["pe", "act", "dve", "pool", "sp"]
        sems = {e: es.enter_context(nc.semaphore("sem_" + e)) for e in eng_names}
        ch_names = (["w%d" % i for i in range(NB)] + ["cv%d" % i for i in range(8)] +
                    ["xs0", "xs1", "os0", "os1", "os2", "misc", "bias"])
        ch_sems = {c: es.enter_context(nc.semaphore("ch_" + c)) for c in ch_names}

        S = Sched(nc)
        V, A, P, G = nc.vector, nc.scalar, nc.tensor, nc.gpsimd

        state = dict(bank=0, tmp=0, sq=0)

        def bank():
            b = state["bank"]
            state["bank"] = (b + 1) % 8
            return b

        def pbank(b, c0=0, n=512):
            return ps[:, 512 * b + c0: 512 * b + c0 + n]

        def tmpi():
            i = state["tmp"]
            state["tmp"] = (i + 1) % NTMP
            return i

        def sqi():
            i = state["sq"]
            state["sq"] = (i + 1) % NSQ
            return i

        def mm(out, lhsT, rhs, start, stop, reads, writes):
            S.op("pe", lambda: P.matmul(out, lhsT, rhs, start=start, stop=stop), reads, writes)

        def act(out, in_, func, reads, writes, bias=None, scale=None, accum_out=None):
            kw = {}
            if bias is not None:
                kw["bias"] = bias
            if scale is not None:
                kw["scale"] = scale
            if accum_out is not None:
                kw["accum_out"] = accum_out
            S.op("act", lambda: A.activation(out, in_, func, **kw), reads, writes)

        def dve(fn, reads, writes):
            S.op("dve", fn, reads, writes)

        S.dma("sp", identf[:], dr["ident"], "misc", (), ["identf"])
        dve(lambda: V.tensor_copy(ident[:], identf[:]), ["identf"], ["ident"])
        dve(lambda: V.memset(onesm[:], 1.0 / 128.0), (), ["onesm"])
        dve(lambda: V.memset(ones1[:], 1.0), (), ["ones1"])
        S.dma("sp", vecs[:], dr["vecs"], "misc", (), ["vecs"])
        for l in range(DEPTH):
            dve(lambda l=l: V.tensor_scalar(dv[:, l * NDV + DV_NBG: l * NDV + DV_NBG + 48],
                                            vecs[:, l * NV + V_BG: l * NV + V_BG + 48], -1.0, None, ALU.mult),
                ["vecs"], ["dv"])
            dve(lambda l=l: V.tensor_scalar(dv[:, l * NDV + DV_QG: l * NDV + DV_QG + 1],
                                            vecs[:, l * NV + V_QG: l * NV + V_QG + 1], 128.0 ** -0.5, None, ALU.mult),
                ["vecs"], ["dv"])
            dve(lambda l=l: V.tensor_scalar(dv[:, l * NDV + DV_MQG: l * NDV + DV_MQG + 1],
                                            vecs[:, l * NV + V_MQG: l * NV + V_MQG + 1], 128.0 ** -0.5, None, ALU.mult),
                ["vecs"], ["dv"])

        def vcol(l, c):
            return vecs[:, l * NV + c: l * NV + c + 1]

        def dvcol(l, c):
            return dv[:, l * NDV + c: l * NDV + c + 1]

        cv_prev = {}
        cv_ids = {}
        cvs = dict(gno=0)
        STREAM_G = list(range(NGT, NGL)) + list(range(NGT))

        def emit_conv(l, glist, pace=()):
            groups = _group_table(l)
            for g in glist:
                parts = groups[g]
                c = "cv%d" % (cvs["gno"] % 8)
                ids = []
                dst = wsc[l, g].rearrange("p (kc c) -> p kc c", kc=KC)
                for (src, r0, nr, c0, ncol, kco, dcol) in parts:
                    sv = dr[src][l, r0:r0 + nr, c0:c0 + ncol].rearrange("(kc p) c -> p kc c", p=128)
                    nk = nr // 128
                    ids.append(S.dma("pool", dst[:, kco:kco + nk, dcol:dcol + ncol], sv, c,
                                     (), [("wsc", l, g, len(ids))], extra=tuple(cv_prev.get(c, ())) + tuple(pace),
                                     chain=False))
                cv_prev[c] = tuple(ids)
                cv_ids[(l, g)] = tuple(ids)
                cvs["gno"] += 1

        for li_, l in enumerate(layers):
            if li_ == 0 or NT < 2:
                emit_conv(l, STREAM_G)

        ring = dict(n=0)
        load_plan = []
        for l in layers:
            load_plan += [(l, G_MKV + i) for i in range(NGS)]
            for t in range(NT):
                load_plan += [(l, g) for g in range(NGT)]
        plan_pos = dict(i=0)

        def issue_load():
            i = plan_pos["i"]
            if i >= len(load_plan):
                return
            plan_pos["i"] = i + 1
            l, g = load_plan[i]
            slot = i % NB
            S.dma("sp", wb[:, slot].rearrange("p kc c -> p (kc c)"), wsc[l, g], "w%d" % slot,
                  [("wsc", l, g, j) for j in range(len(_group_table(l)[g]))], [("wb", slot)])

        cons = dict(i=0)

        def next_group(l, g):
            i = cons["i"]
            assert load_plan[i] == (l, g), (load_plan[i], l, g)
            cons["i"] = i + 1
            return i % NB

        for _ in range(NB):
            issue_load()

        def rmsnorm_rows(src_ap, xsl, chn, dst_slices, src_keys=()):
            S.dma("sp", xs[:, xsl, :], src_ap, chn, list(src_keys), [("xs", xsl)])
            rdst = XN[:, dst_slices:dst_slices + 4, :].rearrange("p a b -> p (a b)")
            keys = [("XN", dst_slices + i) for i in range(4)]
            o = 4 * xsl
            sk = [("stat", xsl, i) for i in range(4)]
            act(rdst, xs[:, xsl, :], AF.Square, [("xs", xsl)], keys + [sk[0]], accum_out=stat[:, o:o + 1])
            dve(lambda: V.tensor_scalar(stat[:, o + 1:o + 2], stat[:, o:o + 1], 1.0 / D, EPS, ALU.mult, ALU.add),
                [sk[0]], [sk[1]])
            act(stat[:, o + 2:o + 3], stat[:, o + 1:o + 2], AF.Ln, [sk[1]], [sk[2]])
            act(stat[:, o + 3:o + 4], stat[:, o + 2:o + 3], AF.Exp, [sk[2]], [sk[3]], scale=-0.5)
            dve(lambda: V.scalar_tensor_tensor(rdst, xs[:, xsl, :], stat[:, o + 3:o + 4], grow[:], ALU.mult, ALU.mult),
                [("xs", xsl), sk[3], "grow"], keys)

        def transposes(nsub, ntok):
            xn = XN[:].rearrange("p a b -> p (a b)")
            for kc in range(KC):
                b = bank()
                for s in range(nsub):
                    mm(pbank(b, 128 * s, 128), xn[:, s * D + 128 * kc: s * D + 128 * kc + 128], ident[:],
                       True, True, [("XN", 4 * s + (128 * kc) // 512), "ident"], [("ps", b)])
                if kc % 2 == 0:
                    act(hT[:, kc, 0:ntok], pbank(b, 0, ntok), AF.Copy, [("ps", b)], [("hT", kc)])
                else:
                    dve(lambda b=b, kc=kc: V.tensor_copy(hT[:, kc, 0:ntok], pbank(b, 0, ntok)),
                        [("ps", b)], [("hT", kc)])

        HT_ALL = [("hT", kc) for kc in range(KC)]

        def proj_fm(slot, cb, b, ntok=T):
            for kc in range(KC):
                mm(pbank(b, 0, ntok), wb[:, slot, kc, 128 * cb:128 * cb + 128], hT[:, kc, 0:ntok],
                   kc == 0, kc == KC - 1, [("wb", slot), ("hT", kc)], [("ps", b)])

        def sigmoid_to(dst_ap, src_psum, reads, writes, bias_neg=None):
            if bias_neg is None:
                act(dst_ap, src_psum, AF.Exp, reads, writes, scale=-1.0)
            else:
                act(dst_ap, src_psum, AF.Exp, reads, writes, scale=-1.0, bias=bias_neg)
            act(dst_ap, dst_ap, AF.Ln, writes, writes, bias=1.0)
            act(dst_ap, dst_ap, AF.Exp, writes, writes, scale=-1.0)

        def qk_norm_finish(item):
            (b, gcol, dst_ap, dst_keys, ntok) = item
            si = sqi()
            act(sqb[:, si, 0:ntok], pbank(b, 0, ntok), AF.Square, [("ps", b)], [("sq", si)])
            b2 = bank()
            mm(pbank(b2, 0, ntok), onesm[:], sqb[:, si, 0:ntok], True, True, [("sq", si), "onesm"], [("ps", b2)])
            ti = tmpi()
            act(tmp[:, ti, 0:ntok], pbank(b2, 0, ntok), AF.Ln, [("ps", b2)], [("tmp", ti)], bias=EPS)
            act(tmp[:, ti, 0:ntok], tmp[:, ti, 0:ntok], AF.Exp, [("tmp", ti)], [("tmp", ti)], scale=-0.5)
            dve(lambda: V.scalar_tensor_tensor(dst_ap, pbank(b, 0, ntok), gcol, tmp[:, ti, 0:ntok],
                                               ALU.mult, ALU.mult),
                [("ps", b), ("tmp", ti), "vecs", "dv"], dst_keys)

        def layer_setup(l):
            S.op("dve", lambda: V.memset(ucar[:], 0.0), (), ["ucar"])
            S.dma("sp", grow[:], dr["grows"][2 * l + 1:2 * l + 2, :].partition_broadcast(128), "misc", (), ["grow"])
            for s in range(2):
                rmsnorm_rows(dr["mem"][128 * s:128 * s + 128, :], s, "xs%d" % s, 4 * s)
            transposes(2, 256)
            pend = None
            for hh in range(4):
                if hh % 2 == 0:
                    slot = next_group(l, G_MKV + hh // 2)
                b = bank()
                proj_fm(slot, hh % 2, b, 256)
                item = (b, vcol(l, V_MKG), mkT[:, hh, :], [("mkT", hh)], 256)
                if pend is not None:
                    qk_norm_finish(pend)
                pend = item
                if hh % 2 == 1:
                    issue_load()
            qk_norm_finish(pend)
            for gi in range(2):
                slot = next_group(l, G_MKV + 2 + gi)
                b = bank()
                for mb in range(2):
                    for kc in range(KC):
                        mm(pbank(b, 256 * mb, 256), hT[:, kc, 128 * mb:128 * mb + 128], wb[:, slot, kc, :],
                           kc == 0, kc == KC - 1, [("wb", slot), ("hT", kc)], [("ps", b)])
                issue_load()
                act(mv[:, :, 256 * gi:256 * gi + 256], pbank(b).rearrange("p (a b) -> p a b", a=2), AF.Copy,
                    [("ps", b)], [("mv", gi)])
            S.dma("sp", grow[:], dr["grows"][2 * l:2 * l + 1, :].partition_broadcast(128), "misc", (), ["grow"])
            for h in range(8):
                r = h % 2
                sv = tmp[:, 2 * r:2 * r + 2, :].rearrange("p a b -> p (a b)")[:, 0:640]
                sk = [("tmp", 2 * r), ("tmp", 2 * r + 1)]
                S.dma("sp", sv, dr["biasT"][l, h], "bias", (), sk)
                dve(lambda sv=sv, h=h: V.tensor_copy(BT[:, h, :], sv), sk, [("BT", h)])

        MKT_ALL = [("mkT", i) for i in range(4)]
        MV_ALL = [("mv", 0), ("mv", 1)]

        def pre(l, t, xin, xin_name):
            t0 = t * T
            xin_keys = [("xbuf", xin_name, t, i) for i in range(8)] if xin_name == "x1" else []
            for s in range(NSUB):
                rmsnorm_rows(xin[t0 + 128 * s: t0 + 128 * s + 128, :], s % 2, "xs%d" % (s % 2), 4 * s, xin_keys)

        def tile(l, t, xin, xout, xin_name, xout_name):
            t0 = t * T
            xin_keys = [("xbuf", xin_name, t, i) for i in range(8)] if xin_name == "x1" else []
            cur = t % 2
            prv = 1 - cur

            for j in range(4):
                slot = next_group(l, G_A + 2 * j)
                b_c, b_x = bank(), bank()
                proj_fm(slot, 0, b_c)
                proj_fm(slot, 1, b_x)
                issue_load()
                slot = next_group(l, G_A + 2 * j + 1)
                b_b, b_z = bank(), bank()
                proj_fm(slot, 0, b_b)
                proj_fm(slot, 1, b_z)
                issue_load()
                tc_, tu, ty, tz = tmpi(), tmpi(), tmpi(), tmpi()
                act(tmp[:, tc_, 0:T], pbank(b_c), AF.Copy, [("ps", b_c)], [("tmp", tc_)])
                dve(lambda tu=tu, j=j: V.tensor_copy(tmp[:, tu, 0:2], ucar[:, j, :]), ["ucar"], [("tmp", tu)])
                dve(lambda tu=tu, tc_=tc_, b_x=b_x: V.tensor_tensor(tmp[:, tu, 2:2 + T], tmp[:, tc_, 0:T], pbank(b_x), ALU.mult),
                    [("tmp", tc_), ("ps", b_x), ("tmp", tu)], [("tmp", tu)])
                dve(lambda tu=tu, j=j: V.tensor_copy(ucar[:, j, :], tmp[:, tu, T:T + 2]), [("tmp", tu)], ["ucar"])
                dve(lambda tu=tu, ty=ty, j=j: V.tensor_scalar(tmp[:, ty, 0:T], tmp[:, tu, 0:T], vcol(l, V_CW + j), None, ALU.mult),
                    [("tmp", tu), "vecs"], [("tmp", ty)])
                dve(lambda tu=tu, ty=ty, j=j: V.scalar_tensor_tensor(tmp[:, ty, 0:T], tmp[:, tu, 1:1 + T], vcol(l, V_CW + 4 + j),
                                                                      tmp[:, ty, 0:T], ALU.mult, ALU.add),
                    [("tmp", tu), ("tmp", ty), "vecs"], [("tmp", ty)])
                dve(lambda tu=tu, ty=ty, j=j: V.scalar_tensor_tensor(tmp[:, ty, 0:T], tmp[:, tu, 2:2 + T], vcol(l, V_CW + 8 + j),
                                                                      tmp[:, ty, 0:T], ALU.mult, ALU.add),
                    [("tmp", tu), ("tmp", ty), "vecs"], [("tmp", ty)])
                sigmoid_to(tmp[:, tz, 0:T], pbank(b_z), [("ps", b_z)], [("tmp", tz)])
                dve(lambda tz=tz, b_z=b_z: V.tensor_tensor(tmp[:, tz, 0:T], tmp[:, tz, 0:T], pbank(b_z), ALU.mult),
                    [("tmp", tz), ("ps", b_z)], [("tmp", tz)])
                dve(lambda tz=tz, b_b=b_b: V.tensor_tensor(tmp[:, tz, 0:T], tmp[:, tz, 0:T], pbank(b_b), ALU.mult),
                    [("tmp", tz), ("ps", b_b)], [("tmp", tz)])
                dve(lambda tz=tz, ty=ty, j=j: V.tensor_tensor(yA[:, j, :], tmp[:, ty, 0:T], tmp[:, tz, 0:T], ALU.mult),
                    [("tmp", tz), ("tmp", ty)], [("yA", j)])

            pend = None
            for which, g0 in (("q", G_Q), ("k", G_K)):
                for h in range(8):
                    if h % 2 == 0:
                        slot = next_group(l, g0 + h // 2)
                    b = bank()
                    proj_fm(slot, h % 2, b)
                    if h % 2 == 1:
                        issue_load()
                    if which == "q":
                        item = (b, dvcol(l, DV_QG), qT[:, h, :], [("qT", h, p) for p in range(4)], T)
                    else:
                        item = (b, vcol(l, V_KG), kT[:, cur, h, :], [("kT", cur, h)], T)
                    if pend is not None:
                        qk_norm_finish(pend)
                    pend = item
            qk_norm_finish(pend)

            for i in range(4):
                slot = next_group(l, G_V + i)
                for half in range(2):
                    b = bank()
                    for ss in range(2):
                        s = 2 * half + ss
                        for kc in range(KC):
                            mm(pbank(b, 256 * ss, 256), hT[:, kc, 128 * s:128 * s + 128], wb[:, slot, kc, :],
                               kc == 0, kc == KC - 1, [("wb", slot), ("hT", kc)], [("ps", b)])
                    if half == 1:
                        issue_load()
                    act(vv[:, cur, 2 * half:2 * half + 2, 256 * i:256 * i + 256],
                        pbank(b).rearrange("p (a b) -> p a b", a=2), AF.Copy, [("ps", b)], [("vv", cur, i, half)])

            def silu_block(slot, cb, dst_slice):
                b = bank()
                proj_fm(slot, cb, b)
                ti = tmpi()
                sigmoid_to(tmp[:, ti, 0:T], pbank(b), [("ps", b)], [("tmp", ti)])
                dve(lambda: V.tensor_tensor(R[:, dst_slice, :], tmp[:, ti, 0:T], pbank(b), ALU.mult),
                    [("tmp", ti), ("ps", b)], [("R", dst_slice)])

            for h in range(8):
                if h % 2 == 0:
                    slot = next_group(l, G_BZ + h // 2)
                silu_block(slot, h % 2, h)
                if h % 2 == 1:
                    issue_load()
            pend = None
            for hh in range(4):
                if hh % 2 == 0:
                    slot = next_group(l, G_MQ + hh // 2)
                b = bank()
                proj_fm(slot, hh % 2, b)
                if hh % 2 == 1:
                    issue_load()
                item = (b, dvcol(l, DV_MQG), R[:, 12 + hh, :], [("R", 12 + hh)], T)
                if pend is not None:
                    qk_norm_finish(pend)
                pend = item
            qk_norm_finish(pend)
            for hh in range(4):
                if hh % 2 == 0:
                    slot = next_group(l, G_MZ + hh // 2)
                silu_block(slot, hh % 2, 8 + hh)
                if hh % 2 == 1:
                    issue_load()

            if t > 0:
                regA = [(prv, 3, 0, 4, 0, 0), (prv, 2, 0, 3, 512, 512), (prv, 0, 0, 1, 896, 896), (prv, 1, 0, 2, 1024, 1024)]
                rA = [-1, -2, -4, -3]
            else:
                regA, rA = [], []
            regB = [(cur, 0, 0, 4, 0, 1536), (cur, 1, 1, 3, 512, 2048), (cur, 3, 3, 1, 896, 2432), (cur, 2, 2, 2, 1024, 1280)]
            rB = [0, 1, 3, 2]
            regions = []
            if t > 0:
                regions.append((0, regA, rA))
            regions.append((1, regB, rB))
            QK_ALL = lambda h: [("qT", h, p) for p in range(4)]

            def stg(ri):
                return tmp[:, 3 * ri:3 * ri + 3, :].rearrange("p a b -> p (a b)")

            def stg_keys(ri):
                return [("tmp", 3 * ri + i) for i in range(3)]

            def reg_banks(ri):
                return [("ps", 3 * ri + i) for i in range(3)]

            def att_S(h):
                for (ri, blks, rs) in regions:
                    for (sl, blk, plo, npr, off, pc) in blks:
                        mm(ps[:, pc: pc + 128 * npr], kT[:, sl, h, 128 * blk:128 * blk + 128],
                           qT[:, h, 128 * plo:128 * (plo + npr)], True, True,
                           [("kT", sl, h)] + QK_ALL(h), [("ps", pc // 512)])

            def att_EW(h):
                hb = h % 2
                for (ri, blks, rs) in regions:
                    st = stg(ri)
                    for (sl, blk, plo, npr, off, pc), r in zip(blks, rs):
                        kb0 = plo - r
                        dve(lambda st=st, pc=pc, off=off, npr=npr, kb0=kb0: V.tensor_tensor(
                            st[:, off:off + 128 * npr], ps[:, pc: pc + 128 * npr],
                            BT[:, h, 128 * kb0:128 * (kb0 + npr)], ALU.add),
                            [("ps", pc // 512), ("BT", h)], [("tmp", 3 * ri + i) for i in range(3)])
                    act(PT[:, hb, 1280 * ri:1280 * ri + 1280], st[:, 0:1280], AF.Exp, stg_keys(ri), [("PT", hb, ri)])

            def att_PV(h):
                hb = h % 2
                nblk = sum(len(rg[2]) for rg in regions)
                bO = 5 + (h % 2)
                for which, bk in enumerate((bO, 7)):
                    i = 0
                    for (ri, blks, rs) in regions:
                        for (sl, blk, plo, npr, off, pc) in blks:
                            lhsT = vv[:, sl, blk, 128 * h:128 * h + 128] if which == 0 else ones1[:]
                            rk = [("vv", sl, h // 2, blk // 2)] if which == 0 else ["ones1"]
                            mm(pbank(bk, 128 * plo, 128 * npr), lhsT,
                               PT[:, hb, 1280 * ri + off:1280 * ri + off + 128 * npr],
                               i == 0, i == nblk - 1, rk + [("PT", hb, ri)], [("ps", bk)])
                            i += 1
                ti = 6
                act(tmp[:, ti, 0:T], pbank(7), AF.Ln, [("ps", 7)], [("tmp", ti)])
                act(tmp[:, ti, 0:T], tmp[:, ti, 0:T], AF.Exp, [("tmp", ti)], [("tmp", ti)], scale=-1.0)
                dve(lambda: V.tensor_tensor(tmp[:, ti, 0:T], tmp[:, ti, 0:T], R[:, h, :], ALU.mult),
                    [("tmp", ti), ("R", h)], [("tmp", ti)])
                dve(lambda: V.tensor_tensor(qT[:, h, :], pbank(bO), tmp[:, ti, 0:T], ALU.mult),
                    [("ps", bO), ("tmp", ti)], QK_ALL(h))

            att_S(0)
            att_EW(0)
            for h in range(8):
                if h + 1 < 8:
                    att_S(h + 1)
                    att_EW(h + 1)
                att_PV(h)

            def mem_S(hh):
                pb = hh % 2
                for mb in range(2):
                    b = bank()
                    mm(pbank(b), mkT[:, hh, 128 * mb:128 * mb + 128], R[:, 12 + hh, :], True, True,
                       [("mkT", hh), ("R", 12 + hh)], [("ps", b)])
                    act(PT[:, pb, 1280 * mb:1280 * mb + T], pbank(b), AF.Exp, [("ps", b)], [("PT", pb, mb)])

            def mem_PV(hh):
                pb = hh % 2
                bo, bd = bank(), bank()
                for mb in range(2):
                    mm(pbank(bo), mv[:, mb, 128 * hh:128 * hh + 128], PT[:, pb, 1280 * mb:1280 * mb + T], mb == 0, mb == 1,
                       [("mv", hh // 2), ("PT", pb, mb)], [("ps", bo)])
                for mb in range(2):
                    mm(pbank(bd), ones1[:], PT[:, pb, 1280 * mb:1280 * mb + T], mb == 0, mb == 1,
                       [("PT", pb, mb), "ones1"], [("ps", bd)])
                ti = tmpi()
                act(tmp[:, ti, 0:T], pbank(bd), AF.Ln, [("ps", bd)], [("tmp", ti)])
                act(tmp[:, ti, 0:T], tmp[:, ti, 0:T], AF.Exp, [("tmp", ti)], [("tmp", ti)], scale=-1.0)
                dve(lambda: V.tensor_tensor(tmp[:, ti, 0:T], tmp[:, ti, 0:T], R[:, 8 + hh, :], ALU.mult),
                    [("tmp", ti), ("R", 8 + hh)], [("tmp", ti)])
                dve(lambda: V.tensor_tensor(R[:, 12 + hh, :], pbank(bo), tmp[:, ti, 0:T], ALU.mult),
                    [("ps", bo), ("tmp", ti)], [("R", 12 + hh)])

            for n in range(5):
                if n < 4:
                    mem_S(n)
                if n >= 1:
                    mem_PV(n - 1)


            if t + 1 < NT:
                pre(l, t + 1, xin, xin_name)
            for f in range(16):
                slot = next_group(l, G_M + 2 * f)
                b_ga, b_gb = bank(), bank()
                proj_fm(slot, 0, b_ga)
                proj_fm(slot, 1, b_gb)
                issue_load()
                slot = next_group(l, G_M + 2 * f + 1)
                b_gm = bank()
                proj_fm(slot, 0, b_gm)
                b_ya, b_yb, b_ym = bank(), bank(), bank()
                for kc in range(4):
                    mm(pbank(b_ya), wb[:, slot, kc, 128:256], yA[:, kc, :], kc == 0, kc == 3,
                       [("wb", slot), ("yA", kc)], [("ps", b_ya)])
                for kc in range(8):
                    mm(pbank(b_yb), wb[:, slot, 4 + kc, 128:256], qT[:, kc, :], kc == 0, kc == 7,
                       [("wb", slot)] + [("qT", kc, p) for p in range(4)], [("ps", b_yb)])
                for kc in range(4):
                    mm(pbank(b_ym), wb[:, slot, 12 + kc, 128:256], R[:, 12 + kc, :], kc == 0, kc == 3,
                       [("wb", slot), ("R", 12 + kc)], [("ps", b_ym)])
                issue_load()
                tis = []
                for br, (bg, by) in enumerate(((b_ga, b_ya), (b_gb, b_yb), (b_gm, b_ym))):
                    ti = tmpi()
                    tis.append(ti)
                    sigmoid_to(tmp[:, ti, 0:T], pbank(bg), [("ps", bg), "dv"], [("tmp", ti)],
                               bias_neg=dvcol(l, DV_NBG + 16 * br + f))
                    dve(lambda ti=ti, by=by: V.tensor_tensor(tmp[:, ti, 0:T], tmp[:, ti, 0:T], pbank(by), ALU.mult),
                        [("tmp", ti), ("ps", by)], [("tmp", ti)])
                dve(lambda tis=tis: V.tensor_tensor(tmp[:, tis[0], 0:T], tmp[:, tis[0], 0:T], tmp[:, tis[1], 0:T], ALU.add),
                    [("tmp", tis[0]), ("tmp", tis[1])], [("tmp", tis[0])])
                dst = R[:, f, :] if f < 12 else mgx[:, f - 12, :]
                dkey = ("R", f) if f < 12 else ("mgx", f - 12)
                dve(lambda tis=tis, dst=dst: V.tensor_tensor(dst, tmp[:, tis[0], 0:T], tmp[:, tis[2], 0:T], ALU.add),
                    [("tmp", tis[0]), ("tmp", tis[2])], [dkey])

            if t + 1 < NT:
                transposes(NSUB, T)
            for i in range(8):
                slot = next_group(l, G_O + i)
                oi = i % 2
                S.dma("sp", osl[:, oi], xin[t0:t0 + T, GW * i:GW * i + GW].rearrange("(s p) c -> p s c", p=128),
                      "os%d" % oi, xin_keys, [("osl", oi, 0), ("osl", oi, 1)])
                for half in range(2):
                    b = bank()
                    for ss in range(2):
                        s = 2 * half + ss
                        for kc in range(KC):
                            lhs = R[:, kc, 128 * s:128 * s + 128] if kc < 12 else mgx[:, kc - 12, 128 * s:128 * s + 128]
                            lk = ("R", kc) if kc < 12 else ("mgx", kc - 12)
                            mm(pbank(b, 256 * ss, 256), lhs, wb[:, slot, kc, :], kc == 0, kc == KC - 1,
                               [("wb", slot), lk], [("ps", b)])
                    if half == 1:
                        issue_load()
                    dve(lambda b=b, half=half, oi=oi: V.tensor_tensor(
                        osl[:, oi, 2 * half:2 * half + 2, :], pbank(b).rearrange("p (a b) -> p a b", a=2),
                        osl[:, oi, 2 * half:2 * half + 2, :], ALU.add),
                        [("ps", b), ("osl", oi, half)], [("osl", oi, half)])
                S.dma("sp", xout[t0:t0 + T, GW * i:GW * i + GW].rearrange("(s p) c -> p s c", p=128), osl[:, oi],
                      "os%d" % oi, [("osl", oi, 0), ("osl", oi, 1)], [("xbuf", xout_name, t, i)])

        nl = len(layers)
        for li, l in enumerate(layers):
            xin, xin_name = (dr["x"], "x") if li == 0 else (x1, "x1")
            xout, xout_name = (dr["out"], "out") if li == nl - 1 else (x1, "x1")
            layer_setup(l)
            pre(l, 0, xin, xin_name)
            transposes(NSUB, T)
            for t in range(NT):
                tile(l, t, xin, xout, xin_name, xout_name)
                if li + 1 < nl and NT >= 2:
                    nch = max(1, NT - 2)
                    if t < nch:
                        per = (len(STREAM_G) + nch - 1) // nch
                        emit_conv(layers[li + 1], STREAM_G[t * per:(t + 1) * per], pace=(len(S.ops) - 1,))
        S.op("sp", lambda: nc.sync.nop(), [("xbuf", "out", t, i) for t in range(NT) for i in range(8)], ())
        S.emit(sems, ch_sems)
    return nc


def _host_prep(inputs, b, SEQ):
    f = np.float32
    m = {}
    m["x"] = np.ascontiguousarray(inputs["x"][b, :SEQ], dtype=f)
    m["mem"] = np.ascontiguousarray(inputs["mem"][b], dtype=f)
    return m


def _shared_prep(inputs, lsel=(0, 1)):
    f = np.float32
    sh = {}
    NL = len(lsel)
    for k in ("w_in", "w_mem_kv", "w_branch_a", "w_branch_b", "w_branch_m", "w_out"):
        sh[k] = np.ascontiguousarray(np.asarray(inputs[k])[list(lsel)], dtype=f)
    vecs = np.zeros((128, NL * NV), f)
    grows = np.zeros((NL * 2, D), f)
    inputs = {k: np.asarray(inputs[k])[list(lsel)] for k in ("b_gate", "conv_w", "q_norm_g", "k_norm_g", "mq_norm_g",
                                                              "mk_norm_g", "norm_g", "mem_norm_g", "rel_table")}
    for l in range(NL):
        bg = np.asarray(inputs["b_gate"][l], f).reshape(3, 16, 128)
        vecs[:, l * NV + V_BG: l * NV + V_BG + 48] = bg.transpose(2, 0, 1).reshape(128, 48)
        cw = np.asarray(inputs["conv_w"][l], f).reshape(3, 4, 128)
        vecs[:, l * NV + V_CW: l * NV + V_CW + 12] = cw.transpose(2, 0, 1).reshape(128, 12)
        vecs[:, l * NV + V_QG] = np.asarray(inputs["q_norm_g"][l], f)
        vecs[:, l * NV + V_KG] = np.asarray(inputs["k_norm_g"][l], f)
        vecs[:, l * NV + V_MQG] = np.asarray(inputs["mq_norm_g"][l], f)
        vecs[:, l * NV + V_MKG] = np.asarray(inputs["mk_norm_g"][l], f)
        grows[2 * l] = np.asarray(inputs["norm_g"][l], f)
        grows[2 * l + 1] = np.asarray(inputs["mem_norm_g"][l], f)
    sh["vecs"] = vecs
    sh["ident"] = np.eye(128, dtype=f)
    sh["grows"] = grows
    kk = np.arange(128)[:, None, None]
    kb = np.arange(5)[None, :, None]
    qq = np.arange(128)[None, None, :]
    rel = qq - kk + 128 * (4 - kb)
    kc = 2 * (kb - 4) + (kk >= 64)
    qc = (qq >= 64).astype(np.int64)
    valid = (kc <= qc) & (kc >= qc - 8)
    idx = np.clip(rel, -256, 256) + 256
    bt = np.empty((NL, 8, 128, 5, 128), f)
    for l in range(NL):
        tab = np.asarray(inputs["rel_table"][l], f)
        g = tab[idx]
        g = np.where(valid[..., None], g, f(MASKV))
        bt[l] = g.transpose(3, 0, 1, 2)[:, :, ::-1, :]
    sh["biasT"] = np.ascontiguousarray(bt.reshape(NL, 8, 128, 640))
    return sh


_PROG_CACHE = {}
FUSED = True


def _run(nc, inputs, xs_list, lsel):
    sh = _shared_prep(inputs, lsel)
    in_maps = []
    for b, xb in enumerate(xs_list):
        m = {"x": np.ascontiguousarray(xb, dtype=np.float32),
             "mem": np.ascontiguousarray(np.asarray(inputs["mem"])[b], dtype=np.float32)}
        m.update(sh)
        in_maps.append(m)
    res = run_bass_kernel_spmd(nc, in_maps, core_ids=list(range(len(xs_list))))
    return [np.asarray(r["out"]) for r in res.results]


def kernel(**inputs):
    x = np.asarray(inputs["x"])
    B, SEQ, _ = x.shape
    nl = 2 if FUSED else 1
    key = (SEQ, nl)
    if key not in _PROG_CACHE:
        _PROG_CACHE[key] = build_program(SEQ, nl)
    nc = _PROG_CACHE[key]
    xs_list = [x[b] for b in range(B)]
    if FUSED:
        xs_list = _run(nc, inputs, xs_list, (0, 1))
    else:
        for l in range(2):
            xs_list = _run(nc, inputs, xs_list, (l,))
    return np.stack(xs_list, axis=0).astype(np.float32)
```

```python
import numpy as np
import concourse.bass as bass
import concourse.mybir as mybir
from concourse.bass_utils import run_bass_kernel_spmd

F32 = mybir.dt.float32
BF16 = mybir.dt.bfloat16
AF = mybir.ActivationFunctionType
ALU = mybir.AluOpType

D = 2048
KC = 16
T = 512
NSUB = 4
DEPTH = 2
N_IN = 13312
EPS = 1e-6
OFF = dict(a_b=0, a_c=512, a_x=1024, a_z=1536, q=2048, k=3072, v=4096, b_z=5120,
           mq=6144, m_z=6656, g_a=7168, g_b=9216, g_m=11264)
NB = 4
GW = 256
NGT = 68
NGS = 4
NGL = NGT + NGS
MASKV = -100.0
V_BG, V_CW, V_QG, V_KG, V_MQG, V_MKG, NV = 0, 48, 60, 61, 62, 63, 64
DV_NBG, DV_QG, DV_MQG, NDV = 0, 48, 49, 50


class Sched:
    def __init__(self, nc):
        self.nc = nc
        self.ops = []
        self.last_w = {}
        self.readers = {}
        self.ch_count = {}
        self.ch_last = {}

    def _add(self, eng, fn, reads, writes, extra, is_dma=False, ch=None):
        deps = set(extra)
        for k in list(reads) + list(writes):
            w = self.last_w.get(k)
            if w is not None:
                deps.add(w)
        for k in writes:
            for r in self.readers.get(k, ()):
                deps.add(r)
        idx = len(self.ops)
        deps.discard(idx)
        op = dict(eng=eng, fn=fn, deps=deps, is_dma=is_dma, ch=ch, mark=False, val=None)
        if is_dma:
            self.ch_count[ch] = self.ch_count.get(ch, 0) + 1
            op["val"] = 16 * self.ch_count[ch]
        self.ops.append(op)
        for k in writes:
            self.last_w[k] = idx
            self.readers[k] = []
        for k in reads:
            if k not in writes:
                self.readers.setdefault(k, []).append(idx)
        return idx

    def op(self, eng, fn, reads=(), writes=(), extra=()):
        return self._add(eng, fn, reads, writes, extra)

    def dma(self, eng, out, in_, ch, reads=(), writes=(), extra=(), chain=True):
        q = {"sp": self.nc.sync, "act": self.nc.scalar, "pool": self.nc.gpsimd}[eng]
        extra = tuple(extra)
        if chain and ch in self.ch_last:
            extra = extra + (self.ch_last[ch],)
        idx = self._add(eng, lambda: q.dma_start(out=out, in_=in_), reads, writes, extra,
                        is_dma=True, ch=ch)
        self.ch_last[ch] = idx
        return idx

    def emit(self, sems, ch_sems):
        nc = self.nc
        engs = {"pe": nc.tensor, "act": nc.scalar, "dve": nc.vector, "pool": nc.gpsimd, "sp": nc.sync}
        ops = self.ops
        for op in ops:
            for d in op["deps"]:
                dop = ops[d]
                if dop["is_dma"]:
                    continue
                if dop["eng"] == "pe" and op["eng"] == "pe" and not op["is_dma"]:
                    continue
                dop["mark"] = True
        cnt = {e: 0 for e in engs}
        seen = {e: {} for e in engs}
        for op in ops:
            e = op["eng"]
            waits = {}
            for d in op["deps"]:
                dop = ops[d]
                if dop["is_dma"]:
                    key = ("ch", dop["ch"])
                    sem = ch_sems[dop["ch"]]
                    v = dop["val"]
                else:
                    if dop["eng"] == "pe" and e == "pe" and not op["is_dma"]:
                        continue
                    key = ("e", dop["eng"])
                    sem = sems[dop["eng"]]
                    v = dop["val"]
                    assert v is not None
                if v <= seen[e].get(key, 0):
                    continue
                if key not in waits or waits[key][1] < v:
                    waits[key] = (sem, v)
            for key, (sem, v) in waits.items():
                engs[e].wait_ge(sem, v)
                seen[e][key] = v
            ins = op["fn"]()
            if op["is_dma"]:
                ins.then_inc(ch_sems[op["ch"]], 16)
            elif op["mark"]:
                cnt[e] += 1
                op["val"] = cnt[e]
                ins.then_inc(sems[e], 1)
        return cnt


def _group_table(l):
    groups = []

    def win(c0, n, dcol=0):
        return ("w_in", 0, D, c0, n, 0, dcol)

    for j in range(4):
        groups.append([win(OFF["a_c"] + 128 * j, 128, 0), win(OFF["a_x"] + 128 * j, 128, 128)])
        groups.append([win(OFF["a_b"] + 128 * j, 128, 0), win(OFF["a_z"] + 128 * j, 128, 128)])
    for name in ("q", "k", "v", "b_z"):
        for i in range(4):
            groups.append([win(OFF[name] + 256 * i, 256)])
    for name in ("mq", "m_z"):
        for i in range(2):
            groups.append([win(OFF[name] + 256 * i, 256)])
    for f in range(16):
        groups.append([win(OFF["g_a"] + 128 * f, 128, 0), win(OFF["g_b"] + 128 * f, 128, 128)])
        groups.append([win(OFF["g_m"] + 128 * f, 128, 0),
                       ("w_branch_a", 0, 512, 128 * f, 128, 0, 128),
                       ("w_branch_b", 0, 1024, 128 * f, 128, 4, 128),
                       ("w_branch_m", 0, 512, 128 * f, 128, 12, 128)])
    for i in range(8):
        groups.append([("w_out", 0, D, 256 * i, 256, 0, 0)])
    assert len(groups) == NGT
    for i in range(4):
        groups.append([("w_mem_kv", 0, D, 256 * i, 256, 0, 0)])
    return groups


G_A = 0
G_Q, G_K, G_V, G_BZ = 8, 12, 16, 20
G_MQ, G_MZ = 24, 26
G_M = 28
G_O = 60
G_MKV = 68


def build_program(SEQ, NL=2):
    NT = SEQ // T
    layers = tuple(range(NL))
    DEPTH = NL
    nc = bass.Bass("TRN2", target_bir_lowering=False)
    dr = {}
    dr["x"] = nc.dram_tensor("x", [SEQ, D], F32, kind="ExternalInput").ap()
    dr["mem"] = nc.dram_tensor("mem", [256, D], F32, kind="ExternalInput").ap()
    dr["w_in"] = nc.dram_tensor("w_in", [DEPTH, D, N_IN], F32, kind="ExternalInput").ap()
    dr["w_mem_kv"] = nc.dram_tensor("w_mem_kv", [DEPTH, D, 1024], F32, kind="ExternalInput").ap()
    dr["w_branch_a"] = nc.dram_tensor("w_branch_a", [DEPTH, 512, D], F32, kind="ExternalInput").ap()
    dr["w_branch_b"] = nc.dram_tensor("w_branch_b", [DEPTH, 1024, D], F32, kind="ExternalInput").ap()
    dr["w_branch_m"] = nc.dram_tensor("w_branch_m", [DEPTH, 512, D], F32, kind="ExternalInput").ap()
    dr["w_out"] = nc.dram_tensor("w_out", [DEPTH, D, D], F32, kind="ExternalInput").ap()
    dr["vecs"] = nc.dram_tensor("vecs", [128, DEPTH * NV], F32, kind="ExternalInput").ap()
    dr["grows"] = nc.dram_tensor("grows", [DEPTH * 2, D], F32, kind="ExternalInput").ap()
    dr["ident"] = nc.dram_tensor("ident", [128, 128], F32, kind="ExternalInput").ap()
    dr["biasT"] = nc.dram_tensor("biasT", [DEPTH, 8, 128, 5 * 128], F32, kind="ExternalInput").ap()
    dr["out"] = nc.dram_tensor("out", [SEQ, D], F32, kind="ExternalOutput").ap()
    x1 = nc.dram_tensor("x1", [SEQ, D], F32, kind="Internal").ap()
    wsc = nc.dram_tensor("wsc", [DEPTH, NGL, 128, KC * GW], BF16, kind="Internal").ap()

    from contextlib import ExitStack
    with ExitStack() as es:
        def sb(name, shape, dt):
            return es.enter_context(nc.sbuf_tensor(name, shape, dt))

        vecs = sb("vecs_sb", [128, DEPTH * NV], F32)
        dv = sb("dv", [128, DEPTH * NDV], F32)
        grow = sb("grow_sb", [128, D], F32)
        ident = sb("ident_sb", [128, 128], BF16)
        identf = sb("identf", [128, 128], F32)
        onesm = sb("onesm", [128, 128], BF16)
        ones1 = sb("ones1", [128, 128], BF16)
        xs = sb("xs", [128, 2, D], F32)
        R = sb("R", [128, 16, 512], BF16)
        hT = sb("hT", [128, KC, T], BF16)
        XN = sb("XN", [128, 16, 512], BF16)
        kT = sb("kT", [128, 2, 8, T], BF16)
        vv = sb("vv", [128, 2, NSUB, 1024], BF16)
        qT = sb("qT", [128, 8, T], BF16)
        yA = sb("yA", [128, 4, T], BF16)
        BT = sb("BT", [128, 8, 5 * 128], BF16)
        mkT = sb("mkT", [128, 4, 256], BF16)
        mv = sb("mv", [128, 2, 512], BF16)
        wb = sb("wb", [128, NB, KC, GW], BF16)
        NTMP = 7
        tmp = sb("tmp", [128, NTMP, 520], F32)
        NSQ = 2
        sqb = sb("sqb", [128, NSQ, T], BF16)
        PT = sb("PT", [128, 2, 2560], BF16)
        osl = sb("osl", [128, 2, NSUB, GW], F32)
        ucar = sb("ucar", [128, 4, 2], F32)
        mgx = sb("mgx", [128, 4, T], BF16)
        stat = sb("stat", [128, 8], F32)
        ps = es.enter_context(nc.psum_tensor("ps", [128, 8 * 512], F32))

        eng_names = ["pe", "act", "dve", "pool", "sp"]
        sems = {e: es.enter_context(nc.semaphore("sem_" + e)) for e in eng_names}
        ch_names = (["w%d" % i for i in range(NB)] + ["cv%d" % i for i in range(8)] +
                    ["xs0", "xs1", "os0", "os1", "os2", "misc", "bias"])
        ch_sems = {c: es.enter_context(nc.semaphore("ch_" + c)) for c in ch_names}

        S = Sched(nc)
        V, A, P, G = nc.vector, nc.scalar, nc.tensor, nc.gpsimd

        state = dict(bank=0, tmp=0, sq=0)

        def bank():
            b = state["bank"]
            state["bank"] = (b + 1) % 8
            return b

        def pbank(b, c0=0, n=512):
            return ps[:, 512 * b + c0: 512 * b + c0 + n]

        def tmpi():
            i = state["tmp"]
            state["tmp"] = (i + 1) % NTMP
            return i

        def sqi():
            i = state["sq"]
            state["sq"] = (i + 1) % NSQ
            return i

        def mm(out, lhsT, rhs, start, stop, reads, writes):
            S.op("pe", lambda: P.matmul(out, lhsT, rhs, start=start, stop=stop), reads, writes)

        def act(out, in_, func, reads, writes, bias=None, scale=None, accum_out=None):
            kw = {}
            if bias is not None:
                kw["bias"] = bias
            if scale is not None:
                kw["scale"] = scale
            if accum_out is not None:
                kw["accum_out"] = accum_out
            S.op("act", lambda: A.activation(out, in_, func, **kw), reads, writes)

        def dve(fn, reads, writes):
            S.op("dve", fn, reads, writes)

        S.dma("sp", identf[:], dr["ident"], "misc", (), ["identf"])
        dve(lambda: V.tensor_copy(ident[:], identf[:]), ["identf"], ["ident"])
        dve(lambda: V.memset(onesm[:], 1.0 / 128.0), (), ["onesm"])
        dve(lambda: V.memset(ones1[:], 1.0), (), ["ones1"])
        S.dma("sp", vecs[:], dr["vecs"], "misc", (), ["vecs"])
        for l in range(DEPTH):
            dve(lambda l=l: V.tensor_scalar(dv[:, l * NDV + DV_NBG: l * NDV + DV_NBG + 48],
                                            vecs[:, l * NV + V_BG: l * NV + V_BG + 48], -1.0, None, ALU.mult),
                ["vecs"], ["dv"])
            dve(lambda l=l: V.tensor_scalar(dv[:, l * NDV + DV_QG: l * NDV + DV_QG + 1],
                                            vecs[:, l * NV + V_QG: l * NV + V_QG + 1], 128.0 ** -0.5, None, ALU.mult),
                ["vecs"], ["dv"])
            dve(lambda l=l: V.tensor_scalar(dv[:, l * NDV + DV_MQG: l * NDV + DV_MQG + 1],
                                            vecs[:, l * NV + V_MQG: l * NV + V_MQG + 1], 128.0 ** -0.5, None, ALU.mult),
                ["vecs"], ["dv"])

        def vcol(l, c):
            return vecs[:, l * NV + c: l * NV + c + 1]

        def dvcol(l, c):
            return dv[:, l * NDV + c: l * NDV + c + 1]

        cv_prev = {}
        cv_ids = {}
        cvs = dict(gno=0)
        STREAM_G = list(range(NGT, NGL)) + list(range(NGT))

        def emit_conv(l, glist, pace=()):
            groups = _group_table(l)
            for g in glist:
                parts = groups[g]
                c = "cv%d" % (cvs["gno"] % 8)
                ids = []
                dst = wsc[l, g].rearrange("p (kc c) -> p kc c", kc=KC)
                for (src, r0, nr, c0, ncol, kco, dcol) in parts:
                    sv = dr[src][l, r0:r0 + nr, c0:c0 + ncol].rearrange("(kc p) c -> p kc c", p=128)
                    nk = nr // 128
                    ids.append(S.dma("pool", dst[:, kco:kco + nk, dcol:dcol + ncol], sv, c,
                                     (), [("wsc", l, g, len(ids))], extra=tuple(cv_prev.get(c, ())) + tuple(pace),
                                     chain=False))
                cv_prev[c] = tuple(ids)
                cv_ids[(l, g)] = tuple(ids)
                cvs["gno"] += 1

        for li_, l in enumerate(layers):
            if li_ == 0 or NT < 2:
                emit_conv(l, STREAM_G)

        ring = dict(n=0)
        load_plan = []
        for l in layers:
            load_plan += [(l, G_MKV + i) for i in range(NGS)]
            for t in range(NT):
                load_plan += [(l, g) for g in range(NGT)]
        plan_pos = dict(i=0)

        def issue_load():
            i = plan_pos["i"]
            if i >= len(load_plan):
                return
            plan_pos["i"] = i + 1
            l, g = load_plan[i]
            slot = i % NB
            S.dma("sp", wb[:, slot].rearrange("p kc c -> p (kc c)"), wsc[l, g], "w%d" % slot,
                  [("wsc", l, g, j) for j in range(len(_group_table(l)[g]))], [("wb", slot)])

        cons = dict(i=0)

        def next_group(l, g):
            i = cons["i"]
            assert load_plan[i] == (l, g), (load_plan[i], l, g)
            cons["i"] = i + 1
            return i % NB

        for _ in range(NB):
            issue_load()

        def rmsnorm_rows(src_ap, xsl, chn, dst_slices, src_keys=()):
            S.dma("sp", xs[:, xsl, :], src_ap, chn, list(src_keys), [("xs", xsl)])
            rdst = XN[:, dst_slices:dst_slices + 4, :].rearrange("p a b -> p (a b)")
            keys = [("XN", dst_slices + i) for i in range(4)]
            o = 4 * xsl
            sk = [("stat", xsl, i) for i in range(4)]
            act(rdst, xs[:, xsl, :], AF.Square, [("xs", xsl)], keys + [sk[0]], accum_out=stat[:, o:o + 1])
            dve(lambda: V.tensor_scalar(stat[:, o + 1:o + 2], stat[:, o:o + 1], 1.0 / D, EPS, ALU.mult, ALU.add),
                [sk[0]], [sk[1]])
            act(stat[:, o + 2:o + 3], stat[:, o + 1:o + 2], AF.Ln, [sk[1]], [sk[2]])
            act(stat[:, o + 3:o + 4], stat[:, o + 2:o + 3], AF.Exp, [sk[2]], [sk[3]], scale=-0.5)
            dve(lambda: V.scalar_tensor_tensor(rdst, xs[:, xsl, :], stat[:, o + 3:o + 4], grow[:], ALU.mult, ALU.mult),
                [("xs", xsl), sk[3], "grow"], keys)

        def transposes(nsub, ntok):
            xn = XN[:].rearrange("p a b -> p (a b)")
            for kc in range(KC):
                b = bank()
                for s in range(nsub):
                    mm(pbank(b, 128 * s, 128), xn[:, s * D + 128 * kc: s * D + 128 * kc + 128], ident[:],
                       True, True, [("XN", 4 * s + (128 * kc) // 512), "ident"], [("ps", b)])
                if kc % 2 == 0:
                    act(hT[:, kc, 0:ntok], pbank(b, 0, ntok), AF.Copy, [("ps", b)], [("hT", kc)])
                else:
                    dve(lambda b=b, kc=kc: V.tensor_copy(hT[:, kc, 0:ntok], pbank(b, 0, ntok)),
                        [("ps", b)], [("hT", kc)])

        HT_ALL = [("hT", kc) for kc in range(KC)]

        def proj_fm(slot, cb, b, ntok=T):
            for kc in range(KC):
                mm(pbank(b, 0, ntok), wb[:, slot, kc, 128 * cb:128 * cb + 128], hT[:, kc, 0:ntok],
                   kc == 0, kc == KC - 1, [("wb", slot), ("hT", kc)], [("ps", b)])

        def sigmoid_to(dst_ap, src_psum, reads, writes, bias_neg=None):
            if bias_neg is None:
                act(dst_ap, src_psum, AF.Exp, reads, writes, scale=-1.0)
            else:
                act(dst_ap, src_psum, AF.Exp, reads, writes, scale=-1.0, bias=bias_neg)
            act(dst_ap, dst_ap, AF.Ln, writes, writes, bias=1.0)
            act(dst_ap, dst_ap, AF.Exp, writes, writes, scale=-1.0)

        def qk_norm_finish(item):
            (b, gcol, dst_ap, dst_keys, ntok) = item
            si = sqi()
            act(sqb[:, si, 0:ntok], pbank(b, 0, ntok), AF.Square, [("ps", b)], [("sq", si)])
            b2 = bank()
            mm(pbank(b2, 0, ntok), onesm[:], sqb[:, si, 0:ntok], True, True, [("sq", si), "onesm"], [("ps", b2)])
            ti = tmpi()
            act(tmp[:, ti, 0:ntok], pbank(b2, 0, ntok), AF.Ln, [("ps", b2)], [("tmp", ti)], bias=EPS)
            act(tmp[:, ti, 0:ntok], tmp[:, ti, 0:ntok], AF.Exp, [("tmp", ti)], [("tmp", ti)], scale=-0.5)
            dve(lambda: V.scalar_tensor_tensor(dst_ap, pbank(b, 0, ntok), gcol, tmp[:, ti, 0:ntok],
                                               ALU.mult, ALU.mult),
                [("ps", b), ("tmp", ti), "vecs", "dv"], dst_keys)

        def layer_setup(l):
            for h in range(8):
                r = h % 2
                sv = tmp[:, 2 * r:2 * r + 2, :].rearrange("p a b -> p (a b)")[:, 0:640]
                sk = [("tmp", 2 * r), ("tmp", 2 * r + 1)]
                S.dma("sp", sv, dr["biasT"][l, h], "bias", (), sk)
                dve(lambda sv=sv, h=h: V.tensor_copy(BT[:, h, :], sv), sk, [("BT", h)])
            S.op("dve", lambda: V.memset(ucar[:], 0.0), (), ["ucar"])
            S.dma("sp", grow[:], dr["grows"][2 * l + 1:2 * l + 2, :].partition_broadcast(128), "misc", (), ["grow"])
            for s in range(2):
                rmsnorm_rows(dr["mem"][128 * s:128 * s + 128, :], s, "xs%d" % s, 4 * s)
            transposes(2, 256)
            pend = None
            for hh in range(4):
                if hh % 2 == 0:
                    slot = next_group(l, G_MKV + hh // 2)
                b = bank()
                proj_fm(slot, hh % 2, b, 256)
                item = (b, vcol(l, V_MKG), mkT[:, hh, :], [("mkT", hh)], 256)
                if pend is not None:
                    qk_norm_finish(pend)
                pend = item
                if hh % 2 == 1:
                    issue_load()
            qk_norm_finish(pend)
            for gi in range(2):
                slot = next_group(l, G_MKV + 2 + gi)
                b = bank()
                for mb in range(2):
                    for kc in range(KC):
                        mm(pbank(b, 256 * mb, 256), hT[:, kc, 128 * mb:128 * mb + 128], wb[:, slot, kc, :],
                           kc == 0, kc == KC - 1, [("wb", slot), ("hT", kc)], [("ps", b)])
                issue_load()
                act(mv[:, :, 256 * gi:256 * gi + 256], pbank(b).rearrange("p (a b) -> p a b", a=2), AF.Copy,
                    [("ps", b)], [("mv", gi)])
            S.dma("sp", grow[:], dr["grows"][2 * l:2 * l + 1, :].partition_broadcast(128), "misc", (), ["grow"])

        MKT_ALL = [("mkT", i) for i in range(4)]
        MV_ALL = [("mv", 0), ("mv", 1)]

        def pre(l, t, xin, xin_name):
            t0 = t * T
            xin_keys = [("xbuf", xin_name, t, i) for i in range(8)] if xin_name == "x1" else []
            for s in range(NSUB):
                rmsnorm_rows(xin[t0 + 128 * s: t0 + 128 * s + 128, :], s % 2, "xs%d" % (s % 2), 4 * s, xin_keys)

        def tile(l, t, xin, xout, xin_name, xout_name):
            t0 = t * T
            xin_keys = [("xbuf", xin_name, t, i) for i in range(8)] if xin_name == "x1" else []
            cur = t % 2
            prv = 1 - cur

            for j in range(4):
                slot = next_group(l, G_A + 2 * j)
                b_c, b_x = bank(), bank()
                proj_fm(slot, 0, b_c)
                proj_fm(slot, 1, b_x)
                issue_load()
                slot = next_group(l, G_A + 2 * j + 1)
                b_b, b_z = bank(), bank()
                proj_fm(slot, 0, b_b)
                proj_fm(slot, 1, b_z)
                issue_load()
                tc_, tu, ty, tz = tmpi(), tmpi(), tmpi(), tmpi()
                act(tmp[:, tc_, 0:T], pbank(b_c), AF.Copy, [("ps", b_c)], [("tmp", tc_)])
                dve(lambda tu=tu, j=j: V.tensor_copy(tmp[:, tu, 0:2], ucar[:, j, :]), ["ucar"], [("tmp", tu)])
                dve(lambda tu=tu, tc_=tc_, b_x=b_x: V.tensor_tensor(tmp[:, tu, 2:2 + T], tmp[:, tc_, 0:T], pbank(b_x), ALU.mult),
                    [("tmp", tc_), ("ps", b_x), ("tmp", tu)], [("tmp", tu)])
                dve(lambda tu=tu, j=j: V.tensor_copy(ucar[:, j, :], tmp[:, tu, T:T + 2]), [("tmp", tu)], ["ucar"])
                dve(lambda tu=tu, ty=ty, j=j: V.tensor_scalar(tmp[:, ty, 0:T], tmp[:, tu, 0:T], vcol(l, V_CW + j), None, ALU.mult),
                    [("tmp", tu), "vecs"], [("tmp", ty)])
                dve(lambda tu=tu, ty=ty, j=j: V.scalar_tensor_tensor(tmp[:, ty, 0:T], tmp[:, tu, 1:1 + T], vcol(l, V_CW + 4 + j),
                                                                      tmp[:, ty, 0:T], ALU.mult, ALU.add),
                    [("tmp", tu), ("tmp", ty), "vecs"], [("tmp", ty)])
                dve(lambda tu=tu, ty=ty, j=j: V.scalar_tensor_tensor(tmp[:, ty, 0:T], tmp[:, tu, 2:2 + T], vcol(l, V_CW + 8 + j),
                                                                      tmp[:, ty, 0:T], ALU.mult, ALU.add),
                    [("tmp", tu), ("tmp", ty), "vecs"], [("tmp", ty)])
                sigmoid_to(tmp[:, tz, 0:T], pbank(b_z), [("ps", b_z)], [("tmp", tz)])
                dve(lambda tz=tz, b_z=b_z: V.tensor_tensor(tmp[:, tz, 0:T], tmp[:, tz, 0:T], pbank(b_z), ALU.mult),
                    [("tmp", tz), ("ps", b_z)], [("tmp", tz)])
                dve(lambda tz=tz, b_b=b_b: V.tensor_tensor(tmp[:, tz, 0:T], tmp[:, tz, 0:T], pbank(b_b), ALU.mult),
                    [("tmp", tz), ("ps", b_b)], [("tmp", tz)])
                dve(lambda tz=tz, ty=ty, j=j: V.tensor_tensor(yA[:, j, :], tmp[:, ty, 0:T], tmp[:, tz, 0:T], ALU.mult),
                    [("tmp", tz), ("tmp", ty)], [("yA", j)])

            pend = None
            for which, g0 in (("q", G_Q), ("k", G_K)):
                for h in range(8):
                    if h % 2 == 0:
                        slot = next_group(l, g0 + h // 2)
                    b = bank()
                    proj_fm(slot, h % 2, b)
                    if h % 2 == 1:
                        issue_load()
                    if which == "q":
                        item = (b, dvcol(l, DV_QG), qT[:, h, :], [("qT", h, p) for p in range(4)], T)
                    else:
                        item = (b, vcol(l, V_KG), kT[:, cur, h, :], [("kT", cur, h)], T)
                    if pend is not None:
                        qk_norm_finish(pend)
                    pend = item
            qk_norm_finish(pend)

            for i in range(4):
                slot = next_group(l, G_V + i)
                for half in range(2):
                    b = bank()
                    for ss in range(2):
                        s = 2 * half + ss
                        for kc in range(KC):
                            mm(pbank(b, 256 * ss, 256), hT[:, kc, 128 * s:128 * s + 128], wb[:, slot, kc, :],
                               kc == 0, kc == KC - 1, [("wb", slot), ("hT", kc)], [("ps", b)])
                    if half == 1:
                        issue_load()
                    act(vv[:, cur, 2 * half:2 * half + 2, 256 * i:256 * i + 256],
                        pbank(b).rearrange("p (a b) -> p a b", a=2), AF.Copy, [("ps", b)], [("vv", cur, i, half)])

            def silu_block(slot, cb, dst_slice):
                b = bank()
                proj_fm(slot, cb, b)
                ti = tmpi()
                sigmoid_to(tmp[:, ti, 0:T], pbank(b), [("ps", b)], [("tmp", ti)])
                dve(lambda: V.tensor_tensor(R[:, dst_slice, :], tmp[:, ti, 0:T], pbank(b), ALU.mult),
                    [("tmp", ti), ("ps", b)], [("R", dst_slice)])

            for h in range(8):
                if h % 2 == 0:
                    slot = next_group(l, G_BZ + h // 2)
                silu_block(slot, h % 2, h)
                if h % 2 == 1:
                    issue_load()
            pend = None
            for hh in range(4):
                if hh % 2 == 0:
                    slot = next_group(l, G_MQ + hh // 2)
                b = bank()
                proj_fm(slot, hh % 2, b)
                if hh % 2 == 1:
                    issue_load()
                item = (b, dvcol(l, DV_MQG), R[:, 12 + hh, :], [("R", 12 + hh)], T)
                if pend is not None:
                    qk_norm_finish(pend)
                pend = item
            qk_norm_finish(pend)
            for hh in range(4):
                if hh % 2 == 0:
                    slot = next_group(l, G_MZ + hh // 2)
                silu_block(slot, hh % 2, 8 + hh)
                if hh % 2 == 1:
                    issue_load()

            if t > 0:
                regA = [(prv, 3, 0, 4, 0, 0), (prv, 2, 0, 3, 512, 512), (prv, 0, 0, 1, 896, 896), (prv, 1, 0, 2, 1024, 1024)]
                rA = [-1, -2, -4, -3]
            else:
                regA, rA = [], []
            regB = [(cur, 0, 0, 4, 0, 1536), (cur, 1, 1, 3, 512, 2048), (cur, 3, 3, 1, 896, 2432), (cur, 2, 2, 2, 1024, 1280)]
            rB = [0, 1, 3, 2]
            regions = []
            if t > 0:
                regions.append((0, regA, rA))
            regions.append((1, regB, rB))
            QK_ALL = lambda h: [("qT", h, p) for p in range(4)]

            def stg(ri):
                return tmp[:, 3 * ri:3 * ri + 3, :].rearrange("p a b -> p (a b)")

            def stg_keys(ri):
                return [("tmp", 3 * ri + i) for i in range(3)]

            def reg_banks(ri):
                return [("ps", 3 * ri + i) for i in range(3)]

            def att_S(h):
                for (ri, blks, rs) in regions:
                    for (sl, blk, plo, npr, off, pc) in blks:
                        mm(ps[:, pc: pc + 128 * npr], kT[:, sl, h, 128 * blk:128 * blk + 128],
                           qT[:, h, 128 * plo:128 * (plo + npr)], True, True,
                           [("kT", sl, h)] + QK_ALL(h), [("ps", pc // 512)])

            def att_EW(h):
                hb = h % 2
                for (ri, blks, rs) in regions:
                    st = stg(ri)
                    for (sl, blk, plo, npr, off, pc), r in zip(blks, rs):
                        kb0 = plo - r
                        dve(lambda st=st, pc=pc, off=off, npr=npr, kb0=kb0: V.tensor_tensor(
                            st[:, off:off + 128 * npr], ps[:, pc: pc + 128 * npr],
                            BT[:, h, 128 * kb0:128 * (kb0 + npr)], ALU.add),
                            [("ps", pc // 512), ("BT", h)], [("tmp", 3 * ri + i) for i in range(3)])
                    act(PT[:, hb, 1280 * ri:1280 * ri + 1280], st[:, 0:1280], AF.Exp, stg_keys(ri), [("PT", hb, ri)])

            def att_PV(h):
                hb = h % 2
                nblk = sum(len(rg[2]) for rg in regions)
                bO = 5 + (h % 2)
                for which, bk in enumerate((bO, 7)):
                    i = 0
                    for (ri, blks, rs) in regions:
                        for (sl, blk, plo, npr, off, pc) in blks:
                            lhsT = vv[:, sl, blk, 128 * h:128 * h + 128] if which == 0 else ones1[:]
                            rk = [("vv", sl, h // 2, blk // 2)] if which == 0 else ["ones1"]
                            mm(pbank(bk, 128 * plo, 128 * npr), lhsT,
                               PT[:, hb, 1280 * ri + off:1280 * ri + off + 128 * npr],
                               i == 0, i == nblk - 1, rk + [("PT", hb, ri)], [("ps", bk)])
                            i += 1
                ti = 6
                act(tmp[:, ti, 0:T], pbank(7), AF.Ln, [("ps", 7)], [("tmp", ti)])
                act(tmp[:, ti, 0:T], tmp[:, ti, 0:T], AF.Exp, [("tmp", ti)], [("tmp", ti)], scale=-1.0)
                dve(lambda: V.tensor_tensor(tmp[:, ti, 0:T], tmp[:, ti, 0:T], R[:, h, :], ALU.mult),
                    [("tmp", ti), ("R", h)], [("tmp", ti)])
                dve(lambda: V.tensor_tensor(qT[:, h, :], pbank(bO), tmp[:, ti, 0:T], ALU.mult),
                    [("ps", bO), ("tmp", ti)], QK_ALL(h))

            att_S(0)
            att_EW(0)
            for h in range(8):
                if h + 1 < 8:
                    att_S(h + 1)
                    att_EW(h + 1)
                att_PV(h)

            def mem_S(hh):
                pb = hh % 2
                for mb in range(2):
                    b = bank()
                    mm(pbank(b), mkT[:, hh, 128 * mb:128 * mb + 128], R[:, 12 + hh, :], True, True,
                       [("mkT", hh), ("R", 12 + hh)], [("ps", b)])
                    act(PT[:, pb, 1280 * mb:1280 * mb + T], pbank(b), AF.Exp, [("ps", b)], [("PT", pb, mb)])

            def mem_PV(hh):
                pb = hh % 2
                bo, bd = bank(), bank()
                for mb in range(2):
                    mm(pbank(bo), mv[:, mb, 128 * hh:128 * hh + 128], PT[:, pb, 1280 * mb:1280 * mb + T], mb == 0, mb == 1,
                       [("mv", hh // 2), ("PT", pb, mb)], [("ps", bo)])
                for mb in range(2):
                    mm(pbank(bd), ones1[:], PT[:, pb, 1280 * mb:1280 * mb + T], mb == 0, mb == 1,
                       [("PT", pb, mb), "ones1"], [("ps", bd)])
                ti = tmpi()
                act(tmp[:, ti, 0:T], pbank(bd), AF.Ln, [("ps", bd)], [("tmp", ti)])
                act(tmp[:, ti, 0:T], tmp[:, ti, 0:T], AF.Exp, [("tmp", ti)], [("tmp", ti)], scale=-1.0)
                dve(lambda: V.tensor_tensor(tmp[:, ti, 0:T], tmp[:, ti, 0:T], R[:, 8 + hh, :], ALU.mult),
                    [("tmp", ti), ("R", 8 + hh)], [("tmp", ti)])
                dve(lambda: V.tensor_tensor(R[:, 12 + hh, :], pbank(bo), tmp[:, ti, 0:T], ALU.mult),
                    [("ps", bo), ("tmp", ti)], [("R", 12 + hh)])

            for n in range(5):
                if n < 4:
                    mem_S(n)
                if n >= 1:
                    mem_PV(n - 1)


            if t + 1 < NT:
                pre(l, t + 1, xin, xin_name)
            for f in range(16):
                slot = next_group(l, G_M + 2 * f)
                b_ga, b_gb = bank(), bank()
                proj_fm(slot, 0, b_ga)
                proj_fm(slot, 1, b_gb)
                issue_load()
                slot = next_group(l, G_M + 2 * f + 1)
                b_gm = bank()
                proj_fm(slot, 0, b_gm)
                b_ya, b_yb, b_ym = bank(), bank(), bank()
                for kc in range(4):
                    mm(pbank(b_ya), wb[:, slot, kc, 128:256], yA[:, kc, :], kc == 0, kc == 3,
                       [("wb", slot), ("yA", kc)], [("ps", b_ya)])
                for kc in range(8):
                    mm(pbank(b_yb), wb[:, slot, 4 + kc, 128:256], qT[:, kc, :], kc == 0, kc == 7,
                       [("wb", slot)] + [("qT", kc, p) for p in range(4)], [("ps", b_yb)])
                for kc in range(4):
                    mm(pbank(b_ym), wb[:, slot, 12 + kc, 128:256], R[:, 12 + kc, :], kc == 0, kc == 3,
                       [("wb", slot), ("R", 12 + kc)], [("ps", b_ym)])
                issue_load()
                tis = []
                for br, (bg, by) in enumerate(((b_ga, b_ya), (b_gb, b_yb), (b_gm, b_ym))):
                    ti = tmpi()
                    tis.append(ti)
                    sigmoid_to(tmp[:, ti, 0:T], pbank(bg), [("ps", bg), "dv"], [("tmp", ti)],
                               bias_neg=dvcol(l, DV_NBG + 16 * br + f))
                    dve(lambda ti=ti, by=by: V.tensor_tensor(tmp[:, ti, 0:T], tmp[:, ti, 0:T], pbank(by), ALU.mult),
                        [("tmp", ti), ("ps", by)], [("tmp", ti)])
                dve(lambda tis=tis: V.tensor_tensor(tmp[:, tis[0], 0:T], tmp[:, tis[0], 0:T], tmp[:, tis[1], 0:T], ALU.add),
                    [("tmp", tis[0]), ("tmp", tis[1])], [("tmp", tis[0])])
                dst = R[:, f, :] if f < 12 else mgx[:, f - 12, :]
                dkey = ("R", f) if f < 12 else ("mgx", f - 12)
                dve(lambda tis=tis, dst=dst: V.tensor_tensor(dst, tmp[:, tis[0], 0:T], tmp[:, tis[2], 0:T], ALU.add),
                    [("tmp", tis[0]), ("tmp", tis[2])], [dkey])

            if t + 1 < NT:
                transposes(NSUB, T)
            for i in range(8):
                slot = next_group(l, G_O + i)
                oi = i % 2
                S.dma("sp", osl[:, oi], xin[t0:t0 + T, GW * i:GW * i + GW].rearrange("(s p) c -> p s c", p=128),
                      "os%d" % oi, xin_keys, [("osl", oi, 0), ("osl", oi, 1)])
                for half in range(2):
                    b = bank()
                    for ss in range(2):
                        s = 2 * half + ss
                        for kc in range(KC):
                            lhs = R[:, kc, 128 * s:128 * s + 128] if kc < 12 else mgx[:, kc - 12, 128 * s:128 * s + 128]
                            lk = ("R", kc) if kc < 12 else ("mgx", kc - 12)
                            mm(pbank(b, 256 * ss, 256), lhs, wb[:, slot, kc, :], kc == 0, kc == KC - 1,
                               [("wb", slot), lk], [("ps", b)])
                    if half == 1:
                        issue_load()
                    dve(lambda b=b, half=half, oi=oi: V.tensor_tensor(
                        osl[:, oi, 2 * half:2 * half + 2, :], pbank(b).rearrange("p (a b) -> p a b", a=2),
                        osl[:, oi, 2 * half:2 * half + 2, :], ALU.add),
                        [("ps", b), ("osl", oi, half)], [("osl", oi, half)])
                S.dma("sp", xout[t0:t0 + T, GW * i:GW * i + GW].rearrange("(s p) c -> p s c", p=128), osl[:, oi],
                      "os%d" % oi, [("osl", oi, 0), ("osl", oi, 1)], [("xbuf", xout_name, t, i)])

        nl = len(layers)
        for li, l in enumerate(layers):
            xin, xin_name = (dr["x"], "x") if li == 0 else (x1, "x1")
            xout, xout_name = (dr["out"], "out") if li == nl - 1 else (x1, "x1")
            layer_setup(l)
            pre(l, 0, xin, xin_name)
            transposes(NSUB, T)
            for t in range(NT):
                tile(l, t, xin, xout, xin_name, xout_name)
                if li + 1 < nl and NT >= 2:
                    nch = max(1, NT - 2)
                    if t < nch:
                        per = (len(STREAM_G) + nch - 1) // nch
                        emit_conv(layers[li + 1], STREAM_G[t * per:(t + 1) * per], pace=(len(S.ops) - 1,))
        S.op("sp", lambda: nc.sync.nop(), [("xbuf", "out", t, i) for t in range(NT) for i in range(8)], ())
        S.emit(sems, ch_sems)
    return nc


def _host_prep(inputs, b, SEQ):
    f = np.float32
    m = {}
    m["x"] = np.ascontiguousarray(inputs["x"][b, :SEQ], dtype=f)
    m["mem"] = np.ascontiguousarray(inputs["mem"][b], dtype=f)
    return m


def _shared_prep(inputs, lsel=(0, 1)):
    f = np.float32
    sh = {}
    NL = len(lsel)
    for k in ("w_in", "w_mem_kv", "w_branch_a", "w_branch_b", "w_branch_m", "w_out"):
        sh[k] = np.ascontiguousarray(np.asarray(inputs[k])[list(lsel)], dtype=f)
    vecs = np.zeros((128, NL * NV), f)
    grows = np.zeros((NL * 2, D), f)
    inputs = {k: np.asarray(inputs[k])[list(lsel)] for k in ("b_gate", "conv_w", "q_norm_g", "k_norm_g", "mq_norm_g",
                                                              "mk_norm_g", "norm_g", "mem_norm_g", "rel_table")}
    for l in range(NL):
        bg = np.asarray(inputs["b_gate"][l], f).reshape(3, 16, 128)
        vecs[:, l * NV + V_BG: l * NV + V_BG + 48] = bg.transpose(2, 0, 1).reshape(128, 48)
        cw = np.asarray(inputs["conv_w"][l], f).reshape(3, 4, 128)
        vecs[:, l * NV + V_CW: l * NV + V_CW + 12] = cw.transpose(2, 0, 1).reshape(128, 12)
        vecs[:, l * NV + V_QG] = np.asarray(inputs["q_norm_g"][l], f)
        vecs[:, l * NV + V_KG] = np.asarray(inputs["k_norm_g"][l], f)
        vecs[:, l * NV + V_MQG] = np.asarray(inputs["mq_norm_g"][l], f)
        vecs[:, l * NV + V_MKG] = np.asarray(inputs["mk_norm_g"][l], f)
        grows[2 * l] = np.asarray(inputs["norm_g"][l], f)
        grows[2 * l + 1] = np.asarray(inputs["mem_norm_g"][l], f)
    sh["vecs"] = vecs
    sh["ident"] = np.eye(128, dtype=f)
    sh["grows"] = grows
    kk = np.arange(128)[:, None, None]
    kb = np.arange(5)[None, :, None]
    qq = np.arange(128)[None, None, :]
    rel = qq - kk + 128 * (4 - kb)
    kc = 2 * (kb - 4) + (kk >= 64)
    qc = (qq >= 64).astype(np.int64)
    valid = (kc <= qc) & (kc >= qc - 8)
    idx = np.clip(rel, -256, 256) + 256
    bt = np.empty((NL, 8, 128, 5, 128), f)
    for l in range(NL):
        tab = np.asarray(inputs["rel_table"][l], f)
        g = tab[idx]
        g = np.where(valid[..., None], g, f(MASKV))
        bt[l] = g.transpose(3, 0, 1, 2)[:, :, ::-1, :]
    sh["biasT"] = np.ascontiguousarray(bt.reshape(NL, 8, 128, 640))
    return sh


_PROG_CACHE = {}
FUSED = True


def _run(nc, inputs, xs_list, lsel):
    sh = _shared_prep(inputs, lsel)
    in_maps = []
    for b, xb in enumerate(xs_list):
        m = {"x": np.ascontiguousarray(xb, dtype=np.float32),
             "mem": np.ascontiguousarray(np.asarray(inputs["mem"])[b], dtype=np.float32)}
        m.update(sh)
        in_maps.append(m)
    res = run_bass_kernel_spmd(nc, in_maps, core_ids=list(range(len(xs_list))))
    return [np.asarray(r["out"]) for r in res.results]


def kernel(**inputs):
    x = np.asarray(inputs["x"])
    B, SEQ, _ = x.shape
    nl = 2 if FUSED else 1
    key = (SEQ, nl)
    if key not in _PROG_CACHE:
        _PROG_CACHE[key] = build_program(SEQ, nl)
    nc = _PROG_CACHE[key]
    xs_list = [x[b] for b in range(B)]
    if FUSED:
        xs_list = _run(nc, inputs, xs_list, (0, 1))
    else:
        for l in range(2):
            xs_list = _run(nc, inputs, xs_list, (l,))
    return np.stack(xs_list, axis=0).astype(np.float32)
```
